# Optimizing a Trainium2 kernel written in Bass

```python
import jax, jax.numpy as jnp
from jax import lax
import numpy as np

D_MODEL = 2048
BATCH = 2
SEQ = 8192
DEPTH = 4
DEC_BATCH = 2
DEC_SEQ = 4096
PAST_LEN = 128

MIX_WIDTH = D_MODEL
RWKV_WIDTH = MIX_WIDTH // 2
POOL_WIDTH = MIX_WIDTH - RWKV_WIDTH
HEAD_DIM = 64
N_HEADS = RWKV_WIDTH // HEAD_DIM
DECAY_RANK = 64
ICLR_RANK = 64
GATE_RANK = 160
POOL_WINDOWS = (2, 4, 8, 16)
N_POOL_GROUPS = len(POOL_WINDOWS)
POOL_GROUP_WIDTH = POOL_WIDTH // N_POOL_GROUPS
D_FF = 4 * D_MODEL
N_DIR = 2
N_MOD = 6
NORM_EPS = 1e-6
GN_EPS = 64e-5

_R0 = 0
_K0 = RWKV_WIDTH
_V0 = 2 * RWKV_WIDTH
_WD0 = 3 * RWKV_WIDTH
_AD0 = _WD0 + N_DIR * DECAY_RANK
_GD0 = _AD0 + N_DIR * ICLR_RANK
_P0 = _GD0 + GATE_RANK
IN_COLS = _P0 + POOL_WIDTH

kernel_name = 'hymba_rwkv7_pool_adaln_encoder'


def rms_norm(x, g):
    x32 = x.astype(jnp.float32)
    y = x32 * lax.rsqrt(jnp.mean(x32 * x32, axis=-1, keepdims=True) + NORM_EPS)
    return (y * g.astype(jnp.float32)).astype(x.dtype)


def modulate(h, shift, scale):
    return h * (1.0 + scale[:, None, :]) + shift[:, None, :]


def wkv7_bidirectional(r, w, k, v, kk, b):
    B, T = r.shape[0], r.shape[1]

    def to_time_major(z):
        z = z.reshape(B, T, N_DIR, N_HEADS, HEAD_DIM)
        z = jnp.stack([z[:, :, 0], z[:, ::-1, 1]], axis=0)
        return jnp.transpose(z, (2, 0, 1, 3, 4))

    def step(S, inp):
        r_t, w_t, k_t, v_t, kk_t, b_t = inp
        sa = jnp.einsum('dbhvk,dbhk->dbhv', S, kk_t)
        S = S * w_t[..., None, :] - sa[..., None] * b_t[..., None, :] + v_t[..., None] * k_t[..., None, :]
        y = jnp.einsum('dbhvk,dbhk->dbhv', S, r_t)
        return S, y

    S0 = jnp.zeros((N_DIR, B, N_HEADS, HEAD_DIM, HEAD_DIM), jnp.float32)
    xs = (to_time_major(r), to_time_major(w), to_time_major(k),
          to_time_major(v), to_time_major(kk), to_time_major(b))
    _, y = lax.scan(step, S0, xs)
    y_fwd = jnp.transpose(y[:, 0], (1, 0, 2, 3))
    y_bwd = jnp.transpose(y[::-1, 1], (1, 0, 2, 3))
    return y_fwd + y_bwd


def rwkv7_mixer(p, w0, w2, a0, a2, g2, k_k, k_a, r_k, lnx_w, lnx_b):
    B, T = p.shape[0], p.shape[1]
    r = p[..., _R0:_R0 + RWKV_WIDTH]
    k = p[..., _K0:_K0 + RWKV_WIDTH]
    v = p[..., _V0:_V0 + RWKV_WIDTH]
    xw = p[..., _WD0:_AD0].reshape(B, T, N_DIR, DECAY_RANK)
    xa = p[..., _AD0:_GD0].reshape(B, T, N_DIR, ICLR_RANK)
    xg = p[..., _GD0:_P0]
    w_log = -jax.nn.softplus(-(w0 + jnp.einsum('btdr,drc->btdc', jnp.tanh(xw), w2))) - 0.5
    decay = jnp.exp(-jnp.exp(w_log))
    a = jax.nn.sigmoid(a0 + jnp.einsum('btdr,drc->btdc', xa, a2))
    g = jnp.einsum('btr,rc->btc', jax.nn.sigmoid(xg), g2)
    kk = (k * k_k).reshape(B, T, N_HEADS, HEAD_DIM)
    kk = kk * lax.rsqrt(jnp.maximum(jnp.sum(kk * kk, axis=-1, keepdims=True), 1e-24))
    kk = kk.reshape(B, T, RWKV_WIDTH)
    k_dir = k[:, :, None, :] * (1.0 + (a - 1.0) * k_a)
    b_dir = kk[:, :, None, :] * a
    both = lambda z: jnp.broadcast_to(z[:, :, None, :], (B, T, N_DIR, RWKV_WIDTH))
    y = wkv7_bidirectional(both(r), decay, k_dir, both(v), both(kk), b_dir)
    mu = jnp.mean(y, axis=-1, keepdims=True)
    var = jnp.mean(jnp.square(y - mu), axis=-1, keepdims=True)
    gn = ((y - mu) * lax.rsqrt(var + GN_EPS)).reshape(B, T, RWKV_WIDTH) * lnx_w + lnx_b
    rk = (r[:, :, None, :] * k_dir).reshape(B, T, N_DIR, N_HEADS, HEAD_DIM) * r_k
    bonus = jnp.sum(rk, axis=(2, 4))[..., None] * v.reshape(B, T, N_HEADS, HEAD_DIM)
    return (gn + bonus.reshape(B, T, RWKV_WIDTH)) * g


def pool_mixer(p, pool_w, pool_scale):
    B, T = p.shape[0], p.shape[1]
    u = p[..., _P0:_P0 + POOL_WIDTH].reshape(B, T, N_POOL_GROUPS, POOL_GROUP_WIDTH)
    cs = jnp.concatenate([jnp.zeros((B, 1, N_POOL_GROUPS, POOL_GROUP_WIDTH), jnp.float32),
                          jnp.cumsum(u, axis=1)], axis=1)
    t = jnp.arange(T)
    pooled = []
    for gi, win in enumerate(POOL_WINDOWS):
        lo = jnp.clip(t - win // 2, 0, T)
        hi = jnp.clip(t + win // 2, 0, T)
        cg = cs[:, :, gi]
        s = jnp.take(cg, hi, axis=1) - jnp.take(cg, lo, axis=1)
        pooled.append(s / (hi - lo).astype(jnp.float32)[None, :, None])
    pooled = jnp.stack(pooled, axis=2) - u
    mixed = jnp.einsum('btgc,gcd->btgd', pooled, pool_w.astype(jnp.float32))
    return mixed.reshape(B, T, POOL_WIDTH) * pool_scale


def trunk(x, c, ada_w, ada_b, norm1_g, w_in, w0, w2, a0, a2, g2, k_k, k_a, r_k,
          lnx_w, lnx_b, pool_w, pool_scale, w_out, norm2_g, mlp_w1, mlp_w2, final_g):
    for l in range(DEPTH):
        mod = jax.nn.silu(c) @ ada_w[l] + ada_b[l]
        sh1, sc1, gt1, sh2, sc2, gt2 = jnp.split(mod, N_MOD, axis=-1)
        h = modulate(rms_norm(x, norm1_g[l]), sh1, sc1)
        p = (h @ w_in[l]).astype(jnp.float32)
        mix_a = rwkv7_mixer(p, w0[l], w2[l], a0[l], a2[l], g2[l], k_k[l], k_a[l], r_k[l],
                            lnx_w[l], lnx_b[l])
        mix_b = pool_mixer(p, pool_w[l], pool_scale[l])
        mix = jnp.concatenate([mix_a, mix_b], axis=-1).astype(x.dtype)
        x = x + gt1[:, None, :] * (mix @ w_out[l])
        h = modulate(rms_norm(x, norm2_g[l]), sh2, sc2)
        f = jnp.square(jax.nn.relu(h @ mlp_w1[l])) @ mlp_w2[l]
        x = x + gt2[:, None, :] * f
    return rms_norm(x, final_g)


def setup_inputs(seed: int = 0) -> dict:
    key = jax.random.key(seed)
    ks = jax.random.split(key, 32)
    f32 = jnp.float32
    nrm = lambda k, shape, s: jax.random.normal(k, shape, f32) * s
    base_decay = -7.0 + 5.0 * (jnp.arange(RWKV_WIDTH, dtype=f32) / (RWKV_WIDTH - 1)) ** 0.85 + 0.5
    return {
        'x_prompt': nrm(ks[0], (BATCH, SEQ, D_MODEL), 1.0),
        'x_sample': nrm(ks[1], (DEC_BATCH, DEC_SEQ, D_MODEL), 1.0),
        'c_prompt': nrm(ks[2], (BATCH, D_MODEL), 1.0),
        'c_sample': nrm(ks[3], (DEC_BATCH, D_MODEL), 1.0),
        'ada_w': nrm(ks[4], (DEPTH, D_MODEL, N_MOD * D_MODEL), 0.5 * D_MODEL ** -0.5),
        'ada_b': nrm(ks[5], (DEPTH, N_MOD * D_MODEL), 0.02),
        'norm1_g': 1.0 + nrm(ks[6], (DEPTH, D_MODEL), 0.05),
        'w_in': nrm(ks[7], (DEPTH, D_MODEL, IN_COLS), D_MODEL ** -0.5),
        'w0': base_decay + nrm(ks[8], (DEPTH, N_DIR, RWKV_WIDTH), 0.1),
        'w2': nrm(ks[9], (DEPTH, N_DIR, DECAY_RANK, RWKV_WIDTH), 0.5 * DECAY_RANK ** -0.5),
        'a0': nrm(ks[10], (DEPTH, N_DIR, RWKV_WIDTH), 0.1),
        'a2': nrm(ks[11], (DEPTH, N_DIR, ICLR_RANK, RWKV_WIDTH), 0.5 * ICLR_RANK ** -0.5),
        'g2': nrm(ks[12], (DEPTH, GATE_RANK, RWKV_WIDTH), GATE_RANK ** -0.5),
        'k_k': 0.85 + nrm(ks[13], (DEPTH, RWKV_WIDTH), 0.05),
        'k_a': 1.0 + nrm(ks[14], (DEPTH, RWKV_WIDTH), 0.05),
        'r_k': nrm(ks[15], (DEPTH, N_HEADS, HEAD_DIM), 0.1),
        'lnx_w': 1.0 + nrm(ks[16], (DEPTH, RWKV_WIDTH), 0.05),
        'lnx_b': nrm(ks[17], (DEPTH, RWKV_WIDTH), 0.02),
        'pool_w': nrm(ks[18], (DEPTH, N_POOL_GROUPS, POOL_GROUP_WIDTH, POOL_GROUP_WIDTH), POOL_GROUP_WIDTH ** -0.5),
        'pool_scale': 1.0 + nrm(ks[19], (DEPTH, POOL_WIDTH), 0.05),
        'w_out': nrm(ks[20], (DEPTH, MIX_WIDTH, D_MODEL), MIX_WIDTH ** -0.5),
        'norm2_g': 1.0 + nrm(ks[21], (DEPTH, D_MODEL), 0.05),
        'mlp_w1': nrm(ks[22], (DEPTH, D_MODEL, D_FF), D_MODEL ** -0.5),
        'mlp_w2': nrm(ks[23], (DEPTH, D_FF, D_MODEL), D_FF ** -0.5),
        'final_g': 1.0 + nrm(ks[24], (D_MODEL,), 0.05),
    }


def reference(x_prompt, x_sample, c_prompt, c_sample, ada_w, ada_b, norm1_g, w_in, w0, w2,
              a0, a2, g2, k_k, k_a, r_k, lnx_w, lnx_b, pool_w, pool_scale, w_out, norm2_g,
              mlp_w1, mlp_w2, final_g):
    y_prompt = trunk(x_prompt, c_prompt, ada_w, ada_b, norm1_g, w_in, w0, w2, a0, a2, g2,
                     k_k, k_a, r_k, lnx_w, lnx_b, pool_w, pool_scale, w_out, norm2_g,
                     mlp_w1, mlp_w2, final_g)
    y_sample = trunk(x_sample, c_sample, ada_w, ada_b, norm1_g, w_in, w0, w2, a0, a2, g2,
                     k_k, k_a, r_k, lnx_w, lnx_b, pool_w, pool_scale, w_out, norm2_g,
                     mlp_w1, mlp_w2, final_g)
    return (y_prompt, y_sample)
```

```python
import contextlib
import os
import numpy as np
import concourse.bass as bass
import concourse.mybir as mybir
from concourse.bass_utils import run_bass_kernel_spmd

F32 = mybir.dt.float32
BF16 = mybir.dt.bfloat16
AF = mybir.ActivationFunctionType
ALU = mybir.AluOpType
AX = mybir.AxisListType

D = 2048
KC = 16
NPOOL = 24
NW = 6
LD_SCALE = -0.6065306597126334
NORM_EPS = 1e-6
GN_EPS = 64e-5
NDT = F32 if os.environ.get('NEU_F32', '1') == '1' else BF16


class Tr:
    def __init__(self, nc, st):
        self.nc = nc
        self.eng = {'pe': nc.tensor, 'act': nc.scalar, 'dve': nc.vector, 'pool': nc.gpsimd, 'sp': nc.sync}
        self.sems = {}
        self.val = {}
        for e in ['pe', 'act', 'dve', 'pool']:
            self.sems[e] = st.enter_context(nc.semaphore('s_' + e))
            self.val[e] = 0
        self.dq = {'sp': [], 'pool': []}
        for q in self.dq:
            for i in range(NPOOL):
                nm = f'd_{q}_{i}'
                self.sems[nm] = st.enter_context(nc.semaphore(nm))
                self.val[nm] = 0
                self.dq[q].append(nm)
        self.dqi = {q: 0 for q in self.dq}
        self.waited = {e: {} for e in self.eng}
        self.bs = {}
        self.nins = 0

    def _deps(self, reads, writes):
        deps = {}
        for b in reads:
            s = self.bs.get(b)
            if s and s[0]:
                if s[0][1] > deps.get(s[0][0], 0):
                    deps[s[0][0]] = s[0][1]
        for b in writes:
            s = self.bs.get(b)
            if s:
                if s[0] and s[0][1] > deps.get(s[0][0], 0):
                    deps[s[0][0]] = s[0][1]
                for k, v in s[1].items():
                    if v > deps.get(k, 0):
                        deps[k] = v
        return deps

    def _wait(self, e, deps):
        for s, v in deps.items():
            if e == 'pe' and s == 'pe':
                continue
            if self.waited[e].get(s, 0) >= v:
                continue
            self.eng[e].wait_ge(self.sems[s], v)
            self.waited[e][s] = v
            self.nins += 1

    def _commit(self, tok, reads, writes):
        s, v = tok
        for b in reads:
            st = self.bs.setdefault(b, [None, {}])
            if st[1].get(s, 0) < v:
                st[1][s] = v
        for b in writes:
            self.bs[b] = [tok, {}]

    def op(self, e, fn, reads=(), writes=()):
        self._wait(e, self._deps(reads, writes))
        ins = fn()
        self.val[e] += 1
        ins.then_inc(self.sems[e], 1)
        self.nins += 1
        self._commit((e, self.val[e]), reads, writes)

    def dma(self, q, out, in_, reads=(), writes=(), **kw):
        self._wait(q, self._deps(reads, writes))
        nm = self.dq[q][self.dqi[q] % NPOOL]
        self.dqi[q] += 1
        ins = self.eng[q].dma_start(out=out, in_=in_, **kw)
        self.val[nm] += 16
        ins.then_inc(self.sems[nm], 16)
        self.nins += 1
        self._commit((nm, self.val[nm]), reads, writes)

    def barrier(self):
        for e in ['pe', 'act', 'dve', 'pool', 'sp']:
            for s, v in self.val.items():
                if v == 0 or (e == 'pe' and s == 'pe'):
                    continue
                if self.waited[e].get(s, 0) >= v:
                    continue
                self.eng[e].wait_ge(self.sems[s], v)
                self.waited[e][s] = v
        self.bs = {}


def build(NCH, DEPTH):
    T = NCH * 128
    PH = int(os.environ.get('KSTOP', '99'))
    SS = int(os.environ.get('SSTOP', '99'))
    ND = int(os.environ.get('NEUDBG', '99'))
    NB = NCH // 4
    HB = NB // 2
    HC = NCH // 2
    nc = bass.Bass("TRN2", target_bir_lowering=False)

    def din(name, shape, dt=F32):
        return nc.dram_tensor(name, list(shape), dt, kind="ExternalInput").ap()

    def dscr(name, shape, dt):
        return nc.dram_tensor(name, list(shape), dt, kind="Internal").ap()

    x_in = din("x", [T, D])
    cT_in = din("cT", [128, KC, 2])
    cm_in = din("cm", [128, 1])
    rc_in = din("rc", [4, T])
    consts_in = din("consts", [128, 3, 128])
    masks_in = din("masks", [128, 6, 512])
    ada_in = din("ada_w", [DEPTH, 96, 128, 2048])
    adab_in = din("ada_b", [128, DEPTH, 96])
    gains_in = din("gains", [128, DEPTH, 2, KC])
    fg_in = din("final_g", [128, KC])
    win_in = din("w_in", [DEPTH, 36, 128, 2048])
    vec_in = din("vecs", [128, DEPTH, 10, 8])
    w2_in = din("w2", [DEPTH, 128, 2048])
    a2_in = din("a2", [DEPTH, 128, 2048])
    g2_in = din("g2", [DEPTH, 128, 2048])
    poolw_in = din("pool_w", [DEPTH, 128, 2048])
    wout_in = din("w_out", [DEPTH, 16, 128, 2048])
    w1_in = din("w1", [DEPTH, 64, 128, 2048])
    w2m_in = din("w2m", [DEPTH, 64, 128, 2048])
    y_out = nc.dram_tensor("y", [T, D], F32, kind="ExternalOutput").ap()

    xT_s = dscr("xT_s", [KC, 128, T], F32)
    fm_s = dscr("fm_s", [2, 4, 8, 128, T], BF16)
    tm_s = dscr("tm_s", [5, NCH, 128, 1024], BF16)
    bon_s = dscr("bon_s", [8, 128, T], F32)
    g_s = dscr("g_s", [8, 128, T], F32)
    u_s = dscr("u_s", [8, 128, T], F32)
    y_s = dscr("y_s", [2, T, 1024], F32)

    with contextlib.ExitStack() as st:
        tr = Tr(nc, st)
        V, S, P = nc.vector, nc.scalar, nc.tensor

        cnt = {'n': 0}

        def sb(stk, name, shape, dt):
            cnt['n'] += 1
            return stk.enter_context(nc.sbuf_tensor(f"sb{cnt['n']}_" + name, list(shape), dt))

        ps = [st.enter_context(nc.psum_tensor(f"ps{i}", [128, 512], F32)) for i in range(8)]
        psb = [p[:].bitcast(BF16) for p in ps]
        wr = [sb(st, f"wr{i}", [128, KC, 128], BF16) for i in range(NW)]
        cf = sb(st, "cf", [128, 3, 128], F32)
        cb = sb(st, "cb", [128, 3, 128], BF16)
        mk = sb(st, "mk", [128, 6, 512], F32)
        cmt = sb(st, "cmt", [128, 1], F32)
        modT = sb(st, "modT", [128, DEPTH, 96, 2], F32)
        adab = sb(st, "adab", [128, DEPTH, 96], F32)
        gains = sb(st, "gains", [128, DEPTH, 2, KC], F32)
        fgt = sb(st, "fgt", [128, KC], F32)
        vecs = sb(st, "vecs", [128, DEPTH, 10, 8], F32)
        GS = sb(st, "GS", [128, 2, 2, 2, KC], F32)
        GL = sb(st, "GL", [128, 2, 8, NCH], F32)
        w2b = sb(st, "w2b", [128, 2, 1024], BF16)
        a2b = sb(st, "a2b", [128, 2, 1024], BF16)
        g2b = sb(st, "g2b", [128, 2, 1024], BF16)
        pwb = sb(st, "pwb", [128, 4, 2, 256], BF16)
        epst = sb(st, "epst", [128, 2], F32)
        tr.op('dve', lambda: V.memset(epst[:, 0:1], NORM_EPS), writes=['epst'])
        tr.op('dve', lambda: V.memset(epst[:, 1:2], GN_EPS), writes=['epst'])
        ident_b = cb[:, 0, :]
        ident_f = cf[:, 0, :]
        bones_b = cb[:, 1, :]
        mones_b = cb[:, 2, :]

        tr.dma('sp', cf[:], consts_in, writes=['cf'])
        tr.dma('pool', cb[:], consts_in, writes=['cb'])
        tr.dma('sp', mk[:], masks_in, writes=['mk'])
        tr.dma('sp', cmt[:], cm_in, writes=['cmt'])
        tr.dma('sp', adab[:], adab_in, writes=['adab'])
        tr.dma('sp', gains[:], gains_in, writes=['gains'])
        tr.dma('sp', fgt[:], fg_in, writes=['fgt'])
        tr.dma('sp', vecs[:], vec_in, writes=['vecs'])

        wstate = {'i': 0}

        def wload(src):
            i = wstate['i'] % NW
            wstate['i'] += 1
            tr.dma('pool', wr[i][:].rearrange("p a b -> p (a b)"), src, writes=[('wr', i)])
            return i

        with contextlib.ExitStack() as s0:
            cTf = sb(s0, "cTf", [128, KC, 2], F32)
            cTb = sb(s0, "cTb", [128, KC, 2], BF16)
            tr.dma('sp', cTf[:], cT_in, writes=['cTf'])
            tr.op('act', lambda: S.activation(out=cTb[:], in_=cTf[:], func=AF.Silu), reads=['cTf'], writes=['cTb'])
            for l in range(DEPTH):
                for n in range(96):
                    wi = wload(ada_in[l, n])
                    pb = n % 2
                    for kc in range(KC):
                        tr.op('pe', lambda kc=kc: P.matmul(ps[pb][:, 0:2], lhsT=wr[wi][:, kc, :], rhs=cTb[:, kc, :],
                                                           start=(kc == 0), stop=(kc == KC - 1)),
                              reads=[('wr', wi), 'cTb'], writes=[('ps', pb)])
                    tr.op('dve', lambda: V.tensor_scalar(out=modT[:, l, n, :], in0=ps[pb][:, 0:2],
                                                         scalar1=adab[:, l, n:n + 1], scalar2=None, op0=ALU.add),
                          reads=[('ps', pb), 'adab'], writes=['modT'])
            xt0 = [sb(s0, f"xt0_{i}", [128, D], F32) for i in range(2)]
            xo0 = [sb(s0, f"xo0_{i}", [128, KC, 128], F32) for i in range(2)]
            for c in range(NCH):
                b = c % 2
                tr.dma('sp', xt0[b][:], x_in[c * 128:(c + 1) * 128, :], writes=[('xt0', b)])
                for q in range(4):
                    pb = (c * 4 + q) % 8
                    for i in range(4):
                        kc = q * 4 + i
                        tr.op('pe', lambda kc=kc, i=i: P.transpose(out=ps[pb][:, i * 128:(i + 1) * 128],
                                                                   in_=xt0[b][:, kc * 128:(kc + 1) * 128],
                                                                   identity=ident_f),
                              reads=[('xt0', b), 'cf'], writes=[('ps', pb)])
                    eng = 'dve' if q % 2 == 0 else 'act'
                    dst = xo0[b][:, q * 4:(q + 1) * 4, :].rearrange("p a b -> p (a b)")
                    if eng == 'dve':
                        tr.op('dve', lambda: V.tensor_copy(out=dst, in_=ps[pb][:]), reads=[('ps', pb)], writes=[('xo0', b, q)])
                    else:
                        tr.op('act', lambda: S.copy(out=dst, in_=ps[pb][:]), reads=[('ps', pb)], writes=[('xo0', b, q)])
                tr.dma('sp', xT_s[:, :, c * 128:(c + 1) * 128].rearrange("k p t -> p k t"), xo0[b][:],
                       reads=[('xo0', b, q) for q in range(4)], writes=['xT_s'])
            tr.barrier()

        def load_layer_small(l):
            tr.dma('pool', w2b[:].rearrange("p a b -> p (a b)"), w2_in[l], writes=['w2b'])
            tr.dma('pool', a2b[:].rearrange("p a b -> p (a b)"), a2_in[l], writes=['a2b'])
            tr.dma('pool', g2b[:].rearrange("p a b -> p (a b)"), g2_in[l], writes=['g2b'])
            tr.dma('pool', pwb[:].rearrange("p a b c -> p (a b c)"), poolw_in[l], writes=['pwb'])
            for ni in range(2):
                for h in range(2):
                    sc = modT[:, l, (3 * ni + 1) * 16:(3 * ni + 2) * 16, h]
                    sh = modT[:, l, (3 * ni) * 16:(3 * ni + 1) * 16, h]
                    tr.op('dve', lambda: V.scalar_tensor_tensor(out=GS[:, ni, h, 0, :], in0=sc, scalar=1.0,
                                                                in1=gains[:, l, ni, :], op0=ALU.add, op1=ALU.mult),
                          reads=['modT', 'gains'], writes=['GS'])
                    tr.op('dve', lambda: V.tensor_copy(out=GS[:, ni, h, 1, :], in_=sh), reads=['modT'], writes=['GS'])

        def norm_mod(xT, hT, sq, rstd, G, SH, pbank, tag):
            for kc in range(KC):
                b = kc % 2
                tr.op('act', lambda: S.activation(out=sq[b][:], in_=xT[:, kc, :], func=AF.Square),
                      reads=[(tag, 'xT', kc)], writes=[(tag, 'sq', b)])
                tr.op('pe', lambda: P.matmul(ps[pbank][:], lhsT=mones_b, rhs=sq[b][:], start=(kc == 0), stop=(kc == KC - 1)),
                      reads=[(tag, 'sq', b), 'cb'], writes=[('ps', pbank)])
            tr.op('act', lambda: S.activation(out=rstd[:], in_=ps[pbank][:], func=AF.Sqrt, bias=epst[:, 0:1], scale=1.0),
                  reads=[('ps', pbank), 'epst'], writes=[(tag, 'rstd')])
            tr.op('dve', lambda: V.reciprocal(out=rstd[:], in_=rstd[:]), reads=[(tag, 'rstd')], writes=[(tag, 'rstd')])
            if hT is None:
                return
            for kc in range(KC):
                b = kc % 2
                tr.op('dve', lambda: V.tensor_tensor(out=sq_f[b][:], in0=xT[:, kc, :], in1=rstd[:], op=ALU.mult),
                      reads=[(tag, 'xT', kc), (tag, 'rstd')], writes=[(tag, 'sqf', b)])
                tr.op('act', lambda: S.activation(out=hT[:, kc, :], in_=sq_f[b][:], func=AF.Identity,
                                                  scale=G[:, kc:kc + 1], bias=SH[:, kc:kc + 1]),
                      reads=[(tag, 'sqf', b), 'GS'], writes=[(tag, 'hT', kc)])

        for l in range(DEPTH):
            load_layer_small(l)
            VK = lambda kind, j: vecs[:, l, kind, j:j + 1]

            with contextlib.ExitStack() as sa:
                xT = sb(sa, "A_xT", [128, KC, 512], F32)
                hT = sb(sa, "A_hT", [128, KC, 512], BF16)
                sq = [sb(sa, f"A_sq{i}", [128, 512], BF16) for i in range(2)]
                sq_f = [sb(sa, f"A_sqf{i}", [128, 512], F32) for i in range(2)]
                rstd = sb(sa, "A_rstd", [128, 512], F32)
                txw = sb(sa, "A_txw", [128, 512], BF16)
                xab = sb(sa, "A_xab", [128, 512], BF16)
                sxg = sb(sa, "A_sxg", [128, 2, 512], BF16)
                f = {n: sb(sa, "A_" + n, [128, 512], F32) for n in
                     ['r', 'k', 'v', 'kkr', 'kk', 'rn', 'sg', 'a', 't1', 'kd', 'kd0', 'bd', 'ld', 'lg', 'lgp', 'Ep', 'Em',
                      'Epr', 'Eh', 'tmp', 'bon']}
                bfm = {n: [sb(sa, f"A_{n}{i}", [128, 512], BF16) for i in range(2)] for n in ['rt', 'kkt', 'kt', 'bt']}
                khb = sb(sa, "A_khb", [128, 512], BF16)
                bhb = sb(sa, "A_bhb", [128, 512], BF16)
                vb = sb(sa, "A_vb", [128, 512], BF16)
                sqb = sb(sa, "A_sqb", [128, 512], BF16)
                rkb = sb(sa, "A_rkb", [128, 512], BF16)
                bonb = sb(sa, "A_bonb", [128, 512], F32)
                gb = sb(sa, "A_gb", [128, 512], F32)
                tmj = sb(sa, "A_tmj", [128, 5, 4, 128], BF16)
                tot = sb(sa, "A_tot", [128, 4], F32)

                def mm_cols(cbi, pbank):
                    wi = wload(win_in[l, cbi])
                    for kc in range(KC):
                        tr.op('pe', lambda kc=kc: P.matmul(ps[pbank][:], lhsT=wr[wi][:, kc, :], rhs=hT[:, kc, :],
                                                           start=(kc == 0), stop=(kc == KC - 1)),
                              reads=[('wr', wi)] + [('A', 'hT', kc)], writes=[('ps', pbank)])

                for blk in range(NB if PH >= 1 else 0):
                    h = 0 if blk < HB else 1
                    t0 = blk * 512
                    for kc in range(KC):
                        tr.dma('sp', xT[:, kc, :], xT_s[kc, :, t0:t0 + 512], writes=[('A', 'xT', kc)])
                    norm_mod(xT, hT, sq, rstd, GS[:, 0, h, 0, :], GS[:, 0, h, 1, :], 0, 'A')
                    mm_cols(24, 1)
                    tr.op('act', lambda: S.activation(out=txw[:], in_=ps[1][:], func=AF.Tanh), reads=[('ps', 1)], writes=['txw'])
                    mm_cols(25, 2)
                    tr.op('dve', lambda: V.tensor_copy(out=xab[:], in_=ps[2][:]), reads=[('ps', 2)], writes=['xab'])
                    for i in range(2):
                        mm_cols(26 + i, 3)
                        tr.op('act', lambda: S.activation(out=sxg[:, i, :], in_=ps[3][:], func=AF.Sigmoid),
                              reads=[('ps', 3)], writes=[('sxg', i)])
                    for j in range(8):
                        pbk = 1 + j % 3
                        mm_cols(28 + j, pbk)
                        ub = sq_f[j % 2]
                        tr.op('act', lambda: S.copy(out=ub[:], in_=ps[pbk][:]), reads=[('ps', pbk)], writes=[('A', 'sqf', j % 2)])
                        tr.dma('sp', u_s[j, :, t0:t0 + 512], ub[:], reads=[('A', 'sqf', j % 2)], writes=['u_s'])
                    for j in range(8):
                        jb = j % 2
                        cs = slice(j * 128, (j + 1) * 128)
                        for nm, cbi, pbk in (('r', j, 4), ('k', 8 + j, 5), ('v', 16 + j, 6)):
                            mm_cols(cbi, pbk)
                            if nm == 'k':
                                tr.op('dve', lambda: V.tensor_copy(out=f[nm][:], in_=ps[pbk][:]), reads=[('ps', pbk)], writes=[nm])
                            else:
                                tr.op('act', lambda: S.copy(out=f[nm][:], in_=ps[pbk][:]), reads=[('ps', pbk)], writes=[nm])
                        tr.op('pe', lambda: P.matmul(ps[7][:], lhsT=g2b[:, 0, cs], rhs=sxg[:, 0, :], start=True, stop=False),
                              reads=['g2b', ('sxg', 0)], writes=[('ps', 7)])
                        tr.op('pe', lambda: P.matmul(ps[7][:], lhsT=g2b[0:32, 1, cs], rhs=sxg[0:32, 1, :], start=False, stop=True),
                              reads=['g2b', ('sxg', 1)], writes=[('ps', 7)])
                        tr.op('act', lambda: S.copy(out=gb[:], in_=ps[7][:]), reads=[('ps', 7)], writes=['gb'])
                        tr.dma('sp', g_s[j, :, t0:t0 + 512], gb[:], reads=['gb'], writes=['g_s'])
                        tr.op('dve', lambda: V.tensor_scalar(out=f['kkr'][:], in0=f['k'][:], scalar1=VK(0, j), scalar2=None, op0=ALU.mult),
                              reads=['k', 'vecs'], writes=['kkr'])
                        tr.op('act', lambda: S.activation(out=sqb[:], in_=f['kkr'][:], func=AF.Square), reads=['kkr'], writes=['sqb'])
                        tr.op('pe', lambda: P.matmul(ps[7][:], lhsT=bones_b, rhs=sqb[:], start=True, stop=True),
                              reads=['cb', 'sqb'], writes=[('ps', 7)])
                        tr.op('dve', lambda: V.tensor_scalar(out=f['rn'][:], in0=ps[7][:], scalar1=1e-24, scalar2=None,
                                                             op0=ALU.max), reads=[('ps', 7)], writes=['rn'])
                        tr.op('act', lambda: S.activation(out=f['rn'][:], in_=f['rn'][:], func=AF.Sqrt), reads=['rn'], writes=['rn'])
                        tr.op('dve', lambda: V.reciprocal(out=f['rn'][:], in_=f['rn'][:]), reads=['rn'], writes=['rn'])
                        tr.op('dve', lambda: V.tensor_tensor(out=f['kk'][:], in0=f['kkr'][:], in1=f['rn'][:], op=ALU.mult),
                              reads=['kkr', 'rn'], writes=['kk'])
                        tr.op('act', lambda: S.copy(out=vb[:], in_=f['v'][:]), reads=['v'], writes=['vb'])
                        for d in range(2):
                            rs = slice(64 * d, 64 * d + 64)
                            tr.op('pe', lambda: P.matmul(ps[7][:], lhsT=w2b[:, d, cs], rhs=txw[:], start=True, stop=True),
                                  reads=['w2b', 'txw'], writes=[('ps', 7)])
                            tr.op('act', lambda: S.activation(out=f['sg'][:], in_=ps[7][:], func=AF.Sigmoid, bias=VK(6 + d, j), scale=1.0),
                                  reads=[('ps', 7), 'vecs'], writes=['sg'])
                            tr.op('pe', lambda: P.matmul(ps[7][:], lhsT=a2b[:, d, cs], rhs=xab[:], start=True, stop=True),
                                  reads=['a2b', 'xab'], writes=[('ps', 7)])
                            tr.op('act', lambda: S.activation(out=f['a'][:], in_=ps[7][:], func=AF.Sigmoid, bias=VK(8 + d, j), scale=1.0),
                                  reads=[('ps', 7), 'vecs'], writes=['a'])
                            tr.op('dve', lambda: V.tensor_scalar(out=f['ld'][:], in0=f['sg'][:], scalar1=LD_SCALE, scalar2=None, op0=ALU.mult),
                                  reads=['sg'], writes=['ld'])
                            for q in range(4):
                                qs = slice(q * 128, (q + 1) * 128)
                                tr.op('dve', lambda: V.tensor_tensor_scan(out=f['lgp'][:, qs], data0=f['ld'][:, qs], data1=f['ld'][:, qs],
                                                                          initial=0.0, op0=ALU.add, op1=ALU.bypass),
                                      reads=['ld'], writes=[('lgp', q)])
                            tr.op('dve', lambda: V.tensor_copy(out=tot[:], in_=f['lgp'][:].rearrange("p (q t) -> p q t", t=128)[:, :, 127]),
                                  reads=[('lgp', q) for q in range(4)], writes=['tot'])
                            if d == 0:
                                lg = f['lgp']
                                lgk = [('lgp', q) for q in range(4)]
                            else:
                                tr.op('dve', lambda: V.tensor_tensor(out=f['lg'][:], in0=f['ld'][:], in1=f['lgp'][:], op=ALU.subtract),
                                      reads=['ld'] + [('lgp', q) for q in range(4)], writes=['lg'])
                                for q in range(4):
                                    qs = slice(q * 128, (q + 1) * 128)
                                    tr.op('dve', lambda: V.tensor_scalar(out=f['lg'][:, qs], in0=f['lg'][:, qs], scalar1=tot[:, q:q + 1],
                                                                         scalar2=None, op0=ALU.add), reads=['lg', 'tot'], writes=['lg'])
                                lg = f['lg']
                                lgk = ['lg']
                            tr.op('act', lambda: S.activation(out=f['Ep'][:], in_=lg[:], func=AF.Exp), reads=lgk, writes=['Ep'])
                            tr.op('act', lambda: S.activation(out=f['Em'][:], in_=lg[:], func=AF.Exp, scale=-1.0), reads=lgk, writes=['Em'])
                            tr.op('dve', lambda: V.tensor_tensor(out=f['tmp'][:], in0=lg[:], in1=f['ld'][:], op=ALU.subtract),
                                  reads=lgk + ['ld'], writes=['tmp'])
                            tr.op('act', lambda: S.activation(out=f['Epr'][:], in_=f['tmp'][:], func=AF.Exp), reads=['tmp'], writes=['Epr'])
                            for q in range(4):
                                qs = slice(q * 128, (q + 1) * 128)
                                tr.op('act', lambda: S.activation(out=f['Eh'][:, qs], in_=lg[:, qs], func=AF.Exp, scale=-1.0, bias=tot[:, q:q + 1]),
                                      reads=lgk + ['tot'], writes=['Eh'])
                            c0 = blk * 4
                            tr.op('act', lambda: S.activation(out=GL[:, d, j, c0:c0 + 4], in_=tot[:], func=AF.Exp), reads=['tot'], writes=['GL'])
                            tr.op('dve', lambda: V.tensor_scalar(out=f['t1'][:], in0=f['a'][:], scalar1=-1.0, scalar2=VK(1, j),
                                                                 op0=ALU.add, op1=ALU.mult), reads=['a', 'vecs'], writes=['t1'])
                            kdn = 'kd0' if d == 0 else 'kd'
                            tr.op('dve', lambda: V.scalar_tensor_tensor(out=f[kdn][:], in0=f['t1'][:], scalar=1.0, in1=f['k'][:],
                                                                        op0=ALU.add, op1=ALU.mult), reads=['t1', 'k'], writes=[kdn])
                            tr.op('dve', lambda: V.tensor_tensor(out=f['bd'][:], in0=f['kk'][:], in1=f['a'][:], op=ALU.mult),
                                  reads=['kk', 'a'], writes=['bd'])
                            tr.op('dve', lambda: V.tensor_tensor(out=bfm['rt'][d][:], in0=f['r'][:], in1=f['Ep'][:], op=ALU.mult),
                                  reads=['r', 'Ep'], writes=[('rt', d)])
                            tr.op('dve', lambda: V.tensor_tensor(out=bfm['kkt'][d][:], in0=f['kk'][:], in1=f['Epr'][:], op=ALU.mult),
                                  reads=['kk', 'Epr'], writes=[('kkt', d)])
                            tr.op('dve', lambda: V.tensor_tensor(out=bfm['kt'][d][:], in0=f[kdn][:], in1=f['Em'][:], op=ALU.mult),
                                  reads=[kdn, 'Em'], writes=[('kt', d)])
                            tr.op('dve', lambda: V.scalar_tensor_tensor(out=bfm['bt'][d][:], in0=f['bd'][:], scalar=-1.0, in1=f['Em'][:],
                                                                        op0=ALU.mult, op1=ALU.mult), reads=['bd', 'Em'], writes=[('bt', d)])
                            tr.op('dve', lambda: V.tensor_tensor(out=khb[:], in0=f[kdn][:], in1=f['Eh'][:], op=ALU.mult),
                                  reads=[kdn, 'Eh'], writes=['khb'])
                            tr.op('dve', lambda: V.scalar_tensor_tensor(out=bhb[:], in0=f['bd'][:], scalar=-1.0, in1=f['Eh'][:],
                                                                        op0=ALU.mult, op1=ALU.mult), reads=['bd', 'Eh'], writes=['bhb'])
                            for ki, nm in enumerate(['rt', 'kkt', 'kt', 'bt']):
                                tr.dma('sp', fm_s[d, ki, j, :, t0:t0 + 512], bfm[nm][d][:], reads=[(nm, d)], writes=['fm_s'])
                            for si, (src, key) in enumerate(((khb, 'khb'), (bhb, 'bhb'))):
                                for q in range(4):
                                    tr.op('pe', lambda q=q: P.transpose(out=psb[0][:, q * 128:(q + 1) * 128], in_=src[:, q * 128:(q + 1) * 128],
                                                                        identity=ident_b), reads=[key, 'cb'], writes=[('ps', 0)])
                                tr.op('act', lambda: S.copy(out=tmj[:, 2 * d + si, :, :].rearrange("p a b -> p (a b)"), in_=psb[0][:, 0:512]),
                                      reads=[('ps', 0)], writes=[('tmj', 2 * d + si)])
                        for q in range(4):
                            tr.op('pe', lambda q=q: P.transpose(out=psb[0][:, q * 128:(q + 1) * 128], in_=vb[:, q * 128:(q + 1) * 128],
                                                                identity=ident_b), reads=['vb', 'cb'], writes=[('ps', 0)])
                        tr.op('act', lambda: S.copy(out=tmj[:, 4, :, :].rearrange("p a b -> p (a b)"), in_=psb[0][:, 0:512]),
                              reads=[('ps', 0)], writes=[('tmj', 4)])
                        for x in range(5):
                            tr.dma('sp', tm_s[x, blk * 4:blk * 4 + 4, :, cs].rearrange("c t k -> t c k"), tmj[:, x, :, :],
                                   reads=[('tmj', x)], writes=['tm_s'])
                        tr.op('dve', lambda: V.tensor_tensor(out=f['tmp'][:], in0=f['kd0'][:], in1=f['kd'][:], op=ALU.add),
                              reads=['kd0', 'kd'], writes=['tmp'])
                        tr.op('dve', lambda: V.scalar_tensor_tensor(out=rkb[:], in0=f['tmp'][:], scalar=VK(2, j), in1=f['r'][:],
                                                                    op0=ALU.mult, op1=ALU.mult), reads=['tmp', 'r', 'vecs'], writes=['rkb'])
                        tr.op('pe', lambda: P.matmul(ps[7][:], lhsT=bones_b, rhs=rkb[:], start=True, stop=True),
                              reads=['cb', 'rkb'], writes=[('ps', 7)])
                        tr.op('dve', lambda: V.tensor_tensor(out=bonb[:], in0=ps[7][:], in1=f['v'][:], op=ALU.mult),
                              reads=[('ps', 7), 'v'], writes=['bonb'])
                        tr.dma('sp', bon_s[j, :, t0:t0 + 512], bonb[:], reads=['bonb'], writes=['bon_s'])
                tr.barrier()

            with contextlib.ExitStack() as ss:
                fmb = [[sb(ss, f"S_fm{b}_{k}", [128, 8, 128], BF16) for k in range(4)] for b in range(2)]
                tmb = [[sb(ss, f"S_tm{b}_{k}", [128, 1024], BF16) for k in range(3)] for b in range(2)]
                AkkT = sb(ss, "S_AkkT", [128, 16, 128], BF16)
                ArkT = sb(ss, "S_ArkT", [128, 16, 128], BF16)
                ArbT = sb(ss, "S_ArbT", [128, 16, 128], BF16)
                X = [sb(ss, f"S_X{i}", [128, 16, 3, 128], mybir.dt.float32r) for i in range(2)]
                Tb = sb(ss, "S_Tb", [128, 16, 128], BF16)
                Wsb = sb(ss, "S_Wsb", [128, 1024], BF16)
                fz = {kind: [sb(ss, f"S_fz{kind}_{hp}", [128, 8, 128], BF16) for hp in range(2)] for kind in (0, 1, 3)}
                Usb = sb(ss, "S_Usb", [128, 1024], BF16)
                y32 = [sb(ss, f"S_y32{i}", [128, 1024], F32) for i in range(2)]
                H32 = [sb(ss, f"S_H32{d}", [128, 8, 128], F32) for d in range(2)]
                Hbf = [sb(ss, f"S_Hbf{d}", [128, 8, 128], BF16) for d in range(2)]
                for d in range(2):
                    tr.op('dve', lambda: V.memset(H32[d][:], 0.0), writes=[('H32', d)])
                    tr.op('dve', lambda: V.memset(Hbf[d][:], 0.0), writes=[('Hbf', d)])
                RT, KKT, KT, BT = 0, 1, 2, 3
                order = []
                for i in range(NCH):
                    order.append((i, 0))
                    order.append((NCH - 1 - i, 1))
                for it, (c, d) in enumerate(order if PH >= 2 else []):
                    b = it % 2
                    tsl = slice(c * 128, (c + 1) * 128)
                    for k in range(4):
                        tr.dma('sp', fmb[b][k][:], fm_s[d, k, :, :, tsl].rearrange("j p t -> p j t"), writes=[('fmb', b, k)])
                    for k, x in enumerate((2 * d, 2 * d + 1, 4)):
                        tr.dma('sp', tmb[b][k][:], tm_s[x, c], writes=[('tmb', b, k)])
                    fm = fmb[b]
                    khat, bhat, vt = tmb[b]
                    fmk = lambda k: ('fmb', b, k)
                    for kind in (0, 1, 3):
                        for hp in range(2):
                            tr.op('dve' if hp == 0 else 'pool',
                                  lambda: (V if hp == 0 else nc.gpsimd).tensor_scalar(out=fz[kind][hp][:], in0=fm[kind][:], scalar1=cf[:, 1, 64 * hp:64 * hp + 1],
                                                                                      scalar2=None, op0=ALU.mult),
                                  reads=[fmk(kind), 'cf'], writes=[('fz', kind, hp)])
                    if (d == 0 and c == HC) or (d == 1 and c == HC - 1):
                        tr.op('dve', lambda: V.tensor_scalar(out=H32[d][:], in0=H32[d][:], scalar1=cmt[:, 0:1], scalar2=None, op0=ALU.mult),
                              reads=['cmt', ('H32', d)], writes=[('H32', d)])
                        tr.op('act', lambda: S.copy(out=Hbf[d][0:64, :, 0:64], in_=H32[d][0:64, :, 0:64]), reads=[('H32', d)], writes=[('Hbf', d)])
                        tr.op('act', lambda: S.copy(out=Hbf[d][64:128, :, 64:128], in_=H32[d][64:128, :, 64:128]), reads=[('H32', d)], writes=[('Hbf', d)])
                    if SS < 1:
                        continue
                    specs = [(AkkT, 'AkkT', KT, KKT, 3 * d + 0), (None, 'M', BT, KKT, 3 * d + 0), (None, 'N', KKT, BT, 3 * d + 1),
                             (ArkT, 'ArkT', KT, RT, 3 * d + 2), (ArbT, 'ArbT', BT, RT, 3 * d + 2)]
                    for si, (dst, key, li, ri, mi) in enumerate(specs):
                        for g in range(4):
                            pbk = 4 + (si * 4 + g) % 4
                            for hh in range(4):
                                hd = g * 4 + hh
                                j, hp = hd // 2, hd % 2
                                tr.op('pe', lambda: P.matmul(ps[pbk][:, hh * 128:(hh + 1) * 128], lhsT=fm[li][:, j, :], rhs=fz[ri][hp][:, j, :],
                                                             start=True, stop=True),
                                      reads=[fmk(li), ('fz', ri, hp)], writes=[('ps', pbk)])
                            if dst is not None:
                                tr.op('dve', lambda: V.tensor_tensor(out=dst[:, g * 4:(g + 1) * 4, :].rearrange("p a b -> p (a b)"),
                                                                     in0=ps[pbk][:], in1=mk[:, mi, :], op=ALU.mult),
                                      reads=[('ps', pbk), 'mk'], writes=[(key, g)])
                            else:
                                kidx = 2 if key == 'M' else 0
                                tr.op('dve', lambda: V.tensor_tensor(out=X[0][:, g * 4:(g + 1) * 4, kidx, :],
                                                                     in0=ps[pbk][:].rearrange("p (a b) -> p a b", b=128),
                                                                     in1=mk[:, mi, :].rearrange("p (a b) -> p a b", b=128), op=ALU.mult),
                                      reads=[('ps', pbk), 'mk'], writes=[('X', 0, kidx, 2 * g), ('X', 0, kidx, 2 * g + 1)])
                    if SS < 2:
                        continue
                    for g in range(4):
                        tr.op('act', lambda: S.copy(out=X[0][:, g * 4:(g + 1) * 4, 1, :], in_=cf[:, 0:1, :].to_broadcast([128, 4, 128])),
                              reads=['cf'], writes=[('X', 0, 1, 2 * g), ('X', 0, 1, 2 * g + 1)])
                    if SS < 3:
                        continue
                    for r in range(7):
                        po, pn = r % 2, (r + 1) % 2
                        for pp in range(8):
                            ba = 4 + pp % 2
                            bb = 6 + pp % 2
                            pav = ps[ba][:].rearrange("p (h k c) -> p k h c", k=2, h=2)
                            for hh in range(2):
                                hd = pp * 2 + hh
                                tr.op('pe', lambda: P.matmul(ps[ba][:, hh * 256:(hh + 1) * 256], lhsT=X[po][:, hd, 0, :], rhs=X[po][:, hd, 1:3, :], start=True, stop=True),
                                      reads=[('X', po, 0, pp), ('X', po, 1, pp), ('X', po, 2, pp)], writes=[('ps', ba)])
                                if r < 6 and ND >= 3:
                                    tr.op('pe', lambda: P.matmul(ps[bb][:, hh * 256:(hh + 1) * 256], lhsT=X[po][:, hd, 2, :], rhs=X[po][:, hd, 0:2, :],
                                                                 start=True, stop=True),
                                          reads=[('X', po, 0, pp), ('X', po, 1, pp), ('X', po, 2, pp)], writes=[('ps', bb)])
                            hs = slice(pp * 2, pp * 2 + 2)
                            tr.op('dve', lambda: V.tensor_tensor(out=X[pn][:, hs, 1, :], in0=pav[:, 0, :, :], in1=X[po][:, hs, 1, :], op=ALU.add),
                                  reads=[('ps', ba), ('X', po, 1, pp)], writes=[('X', pn, 1, pp)])
                            if r < 6 and ND >= 2:
                                tr.op('dve', lambda: V.tensor_copy(out=X[pn][:, hs, 2, :], in_=pav[:, 1, :, :]), reads=[('ps', ba)], writes=[('X', pn, 2, pp)])
                            if r < 6 and ND >= 3:
                                if os.environ.get('NACT', '1') == '1':
                                    for hh in range(2):
                                        tr.op('act', lambda: S.copy(out=X[pn][:, pp * 2 + hh, 0, :], in_=ps[bb][:, hh * 256:hh * 256 + 128]),
                                              reads=[('ps', bb)], writes=[('X', pn, 0, pp)])
                                else:
                                    tr.op('dve', lambda: V.tensor_copy(out=X[pn][:, hs, 0, :],
                                                                       in_=ps[bb][:].rearrange("p (h k c) -> p h k c", h=2, k=2)[:, :, 0, :]),
                                          reads=[('ps', bb)], writes=[('X', pn, 0, pp)])
                    for g in range(4):
                        tr.op('act', lambda: S.copy(out=Tb[:, g * 4:(g + 1) * 4, :], in_=X[1][:, g * 4:(g + 1) * 4, 1, :]),
                              reads=[('X', 1, 1, 2 * g), ('X', 1, 1, 2 * g + 1)], writes=[('Tb', g)])
                    TT = Tb
                    TTk = lambda g: ('Tb', g)
                    if SS < 4:
                        continue
                    for hd in range(16):
                        j, hp = hd // 2, hd % 2
                        pr = slice(64 * hp, 64 * hp + 64)
                        pbk = hd // 8
                        osl = slice((hd % 8) * 64, (hd % 8) * 64 + 64)
                        tr.op('pe', lambda: P.matmul(ps[pbk][:, osl], lhsT=fm[KKT][:, j, :], rhs=Hbf[d][:, j, 64 * hp:64 * hp + 64],
                                                     start=True, stop=False), reads=[fmk(KKT), ('Hbf', d)], writes=[('ps', pbk)])
                        tr.op('pe', lambda: P.matmul(ps[pbk][:, osl], lhsT=AkkT[:, hd, :], rhs=vt[:, hd * 64:(hd + 1) * 64],
                                                     start=False, stop=True), reads=[('AkkT', hd // 4), ('tmb', b, 2)], writes=[('ps', pbk)])
                    for hf in range(2):
                        tr.op('act', lambda: S.copy(out=Wsb[:, hf * 512:(hf + 1) * 512], in_=ps[hf][:]), reads=[('ps', hf)], writes=[('Wsb', hf)])
                    if SS < 5:
                        continue
                    for hd in range(16):
                        pbk = hd // 8
                        osl = slice((hd % 8) * 64, (hd % 8) * 64 + 64)
                        tr.op('pe', lambda: P.matmul(ps[pbk][:, osl], lhsT=TT[:, hd, :], rhs=Wsb[:, hd * 64:(hd + 1) * 64], start=True, stop=True),
                              reads=[TTk(hd // 4), ('Wsb', hd // 8)], writes=[('ps', pbk)])
                    for hf in range(2):
                        tr.op('dve', lambda: V.tensor_copy(out=Usb[:, hf * 512:(hf + 1) * 512], in_=ps[hf][:]), reads=[('ps', hf)], writes=[('Usb', hf)])
                    if SS < 6:
                        continue
                    for hd in range(16):
                        j, hp = hd // 2, hd % 2
                        pr = slice(64 * hp, 64 * hp + 64)
                        pbk = 2 + hd // 8
                        osl = slice((hd % 8) * 64, (hd % 8) * 64 + 64)
                        tr.op('pe', lambda: P.matmul(ps[pbk][:, osl], lhsT=fm[RT][:, j, :], rhs=Hbf[d][:, j, 64 * hp:64 * hp + 64],
                                                     start=True, stop=False), reads=[fmk(RT), ('Hbf', d)], writes=[('ps', pbk)])
                        tr.op('pe', lambda: P.matmul(ps[pbk][:, osl], lhsT=ArkT[:, hd, :], rhs=vt[:, hd * 64:(hd + 1) * 64],
                                                     start=False, stop=False), reads=[('ArkT', hd // 4), ('tmb', b, 2)], writes=[('ps', pbk)])
                        tr.op('pe', lambda: P.matmul(ps[pbk][:, osl], lhsT=ArbT[:, hd, :], rhs=Usb[:, hd * 64:(hd + 1) * 64],
                                                     start=False, stop=True), reads=[('ArbT', hd // 4), ('Usb', hd // 8)], writes=[('ps', pbk)])
                    yb = y32[it % 2]
                    for hf in range(2):
                        tr.op('act', lambda: S.copy(out=yb[:, hf * 512:(hf + 1) * 512], in_=ps[2 + hf][:]), reads=[('ps', 2 + hf)],
                              writes=[('y32', it % 2, hf)])
                    tr.dma('sp', y_s[d, tsl, :], yb[:], reads=[('y32', it % 2, 0), ('y32', it % 2, 1)], writes=['y_s'])
                    if SS < 7:
                        continue
                    for j in range(8):
                        pbk = j // 4
                        osl = slice((j % 4) * 128, (j % 4) * 128 + 128)
                        cs = slice(j * 128, (j + 1) * 128)
                        tr.op('pe', lambda: P.matmul(ps[pbk][:, osl], lhsT=khat[:, cs], rhs=vt[:, cs], start=True, stop=False),
                              reads=[('tmb', b, 0), ('tmb', b, 2)], writes=[('ps', pbk)])
                        tr.op('pe', lambda: P.matmul(ps[pbk][:, osl], lhsT=bhat[:, cs], rhs=Usb[:, cs], start=False, stop=True),
                              reads=[('tmb', b, 1), ('Usb', j // 4)], writes=[('ps', pbk)])
                    for j in range(8):
                        pbk = j // 4
                        osl = slice((j % 4) * 128, (j % 4) * 128 + 128)
                        tr.op('dve', lambda: V.scalar_tensor_tensor(out=H32[d][:, j, :], in0=H32[d][:, j, :], scalar=GL[:, d, j, c:c + 1],
                                                                    in1=ps[pbk][:, osl], op0=ALU.mult, op1=ALU.add),
                              reads=[('H32', d), 'GL', ('ps', pbk)], writes=[('H32', d)])
                    tr.op('act', lambda: S.copy(out=Hbf[d][0:64, :, 0:64], in_=H32[d][0:64, :, 0:64]), reads=[('H32', d)], writes=[('Hbf', d)])
                    tr.op('act', lambda: S.copy(out=Hbf[d][64:128, :, 64:128], in_=H32[d][64:128, :, 64:128]), reads=[('H32', d)], writes=[('Hbf', d)])
                tr.barrier()

            with contextlib.ExitStack() as sc:
                xT = sb(sc, "C_xT", [128, KC, 512], F32)
                mixT = sb(sc, "C_mixT", [128, KC, 512], BF16)
                yf = [sb(sc, f"C_yf{i}", [128, 1024], F32) for i in range(2)]
                ybk = [sb(sc, f"C_yb{i}", [128, 1024], F32) for i in range(2)]
                ysq = sb(sc, "C_ysq", [128, 1024], F32)
                ynb = [sb(sc, f"C_ynb{i}", [128, 1024], BF16) for i in range(2)]
                st1 = sb(sc, "C_st", [128, 6, 16], F32)
                bon = sb(sc, "C_bon", [128, 8, 512], F32)
                gg = sb(sc, "C_g", [128, 8, 512], F32)
                gnf = [sb(sc, f"C_gnf{i}", [128, 512], F32) for i in range(2)]
                uh = sb(sc, "C_uh", [128, 8, 528], F32)
                sw = [sb(sc, f"C_sw{i}", [128, 528], F32) for i in range(2)]
                rcb = sb(sc, "C_rc", [128, 4, 512], F32)
                Xp = [sb(sc, f"C_Xp{i}", [128, 512], BF16) for i in range(4)]
                for blk in range(NB if PH >= 3 else 0):
                    h = 0 if blk < HB else 1
                    t0 = blk * 512
                    for kc in range(KC):
                        tr.dma('sp', xT[:, kc, :], xT_s[kc, :, t0:t0 + 512], writes=[('C', 'xT', kc)])
                    tr.dma('sp', bon[:], bon_s[:, :, t0:t0 + 512].rearrange("j p t -> p j t"), writes=['bon'])
                    tr.dma('sp', gg[:], g_s[:, :, t0:t0 + 512].rearrange("j p t -> p j t"), writes=['gg'])
                    for q in range(4):
                        b = q % 2
                        tq = slice(t0 + q * 128, t0 + (q + 1) * 128)
                        tr.dma('sp', yf[b][:], y_s[0, tq, :], writes=[('yf', b)])
                        tr.dma('sp', ybk[b][:], y_s[1, tq, :], writes=[('ybk', b)])
                        tr.op('dve', lambda: V.tensor_tensor(out=yf[b][:], in0=yf[b][:], in1=ybk[b][:], op=ALU.add),
                              reads=[('yf', b), ('ybk', b)], writes=[('yf', b)])
                        y3 = yf[b][:].rearrange("p (h v) -> p h v", v=64)
                        tr.op('dve', lambda: V.tensor_reduce(out=st1[:, 0, :], in_=y3, axis=AX.X, op=ALU.add), reads=[('yf', b)], writes=[('st', 0)])
                        tr.op('act', lambda: S.activation(out=ysq[:], in_=yf[b][:], func=AF.Square), reads=[('yf', b)], writes=['ysq'])
                        tr.op('dve', lambda: V.tensor_reduce(out=st1[:, 1, :], in_=ysq[:].rearrange("p (h v) -> p h v", v=64), axis=AX.X, op=ALU.add),
                              reads=['ysq'], writes=[('st', 1)])
                        tr.op('dve', lambda: V.tensor_scalar(out=st1[:, 2, :], in0=st1[:, 0, :], scalar1=1.0 / 64, scalar2=None, op0=ALU.mult),
                              reads=[('st', 0)], writes=[('st', 2)])
                        tr.op('dve', lambda: V.tensor_tensor(out=st1[:, 3, :], in0=st1[:, 2, :], in1=st1[:, 2, :], op=ALU.mult),
                              reads=[('st', 2)], writes=[('st', 3)])
                        tr.op('dve', lambda: V.scalar_tensor_tensor(out=st1[:, 4, :], in0=st1[:, 1, :], scalar=1.0 / 64, in1=st1[:, 3, :],
                                                                    op0=ALU.mult, op1=ALU.subtract), reads=[('st', 1), ('st', 3)], writes=[('st', 4)])
                        tr.op('act', lambda: S.activation(out=st1[:, 5, :], in_=st1[:, 4, :], func=AF.Sqrt, bias=epst[:, 1:2], scale=1.0),
                              reads=[('st', 4), 'epst'], writes=[('st', 5)])
                        tr.op('dve', lambda: V.reciprocal(out=st1[:, 5, :], in_=st1[:, 5, :]), reads=[('st', 5)], writes=[('st', 5)])
                        tr.op('dve', lambda: V.tensor_tensor(out=y3, in0=y3, in1=st1[:, 2, :].unsqueeze(2).to_broadcast([128, 16, 64]), op=ALU.subtract),
                              reads=[('yf', b), ('st', 2)], writes=[('yf', b)])
                        tr.op('dve', lambda: V.tensor_tensor(out=ynb[b][:].rearrange("p (h v) -> p h v", v=64), in0=y3,
                                                             in1=st1[:, 5, :].unsqueeze(2).to_broadcast([128, 16, 64]), op=ALU.mult),
                              reads=[('yf', b), ('st', 5)], writes=[('ynb', b)])
                        for j in range(8):
                            pbk = j // 2
                            off = (j % 2) * 512 + q * 128
                            tr.op('pe', lambda: P.transpose(out=psb[pbk][:, off:off + 128], in_=ynb[b][:, j * 128:(j + 1) * 128], identity=ident_b),
                                  reads=[('ynb', b), 'cb'], writes=[('psq', pbk, q)])
                    for j in range(8):
                        pbk = j // 2
                        gb_ = gnf[j % 2]
                        src = psb[pbk][:, (j % 2) * 512:(j % 2) * 512 + 512]
                        tr.op('act', lambda: S.activation(out=gb_[:], in_=src, func=AF.Identity, scale=VK(3, j), bias=VK(4, j)),
                              reads=[('psq', pbk, q) for q in range(4)] + ['vecs'], writes=[('gnf', j % 2)])
                        tr.op('dve', lambda: V.tensor_tensor(out=gb_[:], in0=gb_[:], in1=bon[:, j, :], op=ALU.add),
                              reads=[('gnf', j % 2), 'bon'], writes=[('gnf', j % 2)])
                        tr.op('dve', lambda: V.tensor_tensor(out=mixT[:, j, :], in0=gb_[:], in1=gg[:, j, :], op=ALU.mult),
                              reads=[('gnf', j % 2), 'gg'], writes=[('C', 'mixT', j)])
                    for pbk in range(4):
                        for q in range(4):
                            pass
                    lo = t0 - 8
                    hi = t0 + 520
                    if blk == 0:
                        tr.op('dve', lambda: V.memset(uh[:, :, 0:8], 0.0), writes=['uhL'])
                        lo = t0
                    if blk == NB - 1:
                        tr.op('dve', lambda: V.memset(uh[:, :, 520:528], 0.0), writes=['uhR'])
                        hi = t0 + 512
                    tr.dma('sp', uh[:, :, 8 + (lo - t0):8 + (hi - t0)], u_s[:, :, lo:hi].rearrange("j p t -> p j t"),
                           writes=['uhL', 'uh', 'uhR'])
                    if blk == HB and blk > 0:
                        tr.op('dve', lambda: V.tensor_scalar(out=uh[:, :, 0:8], in0=uh[:, :, 0:8], scalar1=cmt[:, 0:1], scalar2=None, op0=ALU.mult),
                              reads=['uhL', 'cmt'], writes=['uhL'])
                    if blk == HB - 1 and blk < NB - 1:
                        tr.op('dve', lambda: V.tensor_scalar(out=uh[:, :, 520:528], in0=uh[:, :, 520:528], scalar1=cmt[:, 0:1], scalar2=None, op0=ALU.mult),
                              reads=['uhR', 'cmt'], writes=['uhR'])
                    tr.dma('sp', rcb[:], rc_in[:, t0:t0 + 512].partition_broadcast(128), writes=['rcb'])
                    for j in range(8):
                        gi = j // 2
                        w = 2 << gi
                        cur = uh[:, j, :]
                        curk = ['uhL', 'uh', 'uhR']
                        n = 528
                        stp = 1
                        k = 0
                        while stp < w:
                            dst = sw[k % 2]
                            tr.op('dve', lambda cur=cur, dst=dst, n=n, stp=stp: V.tensor_tensor(out=dst[:, 0:n - stp], in0=cur[:, 0:n - stp],
                                                                                               in1=cur[:, stp:n], op=ALU.add),
                                  reads=curk, writes=[('sw', k % 2)])
                            cur = dst[:]
                            curk = [('sw', k % 2)]
                            n -= stp
                            stp *= 2
                            k += 1
                        o = 8 - w // 2
                        xp = Xp[j % 4]
                        tmpb = sw[k % 2]
                        tr.op('dve', lambda: V.tensor_tensor(out=tmpb[:, 0:512], in0=cur[:, o:o + 512], in1=rcb[:, gi, :], op=ALU.mult),
                              reads=curk + ['rcb'], writes=[('sw', k % 2)])
                        tr.op('dve', lambda: V.tensor_tensor(out=xp[:], in0=tmpb[:, 0:512], in1=uh[:, j, 8:520], op=ALU.subtract),
                              reads=[('sw', k % 2), 'uh'], writes=[('Xp', j % 4)])
                        if j % 2 == 1:
                            for dc in range(2):
                                pbk = 4 + dc
                                for kc2 in range(2):
                                    tr.op('pe', lambda: P.matmul(ps[pbk][:], lhsT=pwb[:, gi, kc2, dc * 128:(dc + 1) * 128], rhs=Xp[(j - 1 + kc2) % 4][:],
                                                                 start=(kc2 == 0), stop=(kc2 == 1)),
                                          reads=['pwb', ('Xp', (j - 1 + kc2) % 4)], writes=[('ps', pbk)])
                                jj = 2 * gi + dc
                                tr.op('act', lambda: S.activation(out=mixT[:, 8 + jj, :], in_=ps[pbk][:], func=AF.Copy, scale=VK(5, jj)),
                                      reads=[('ps', pbk), 'vecs'], writes=[('C', 'mixT', 8 + jj)])
                    for m in range(16):
                        wi = wload(wout_in[l, m])
                        pbk = 6 + m % 2
                        for kc in range(KC):
                            tr.op('pe', lambda kc=kc: P.matmul(ps[pbk][:], lhsT=wr[wi][:, kc, :], rhs=mixT[:, kc, :], start=(kc == 0), stop=(kc == KC - 1)),
                                  reads=[('wr', wi), ('C', 'mixT', kc)], writes=[('ps', pbk)])
                        tr.op('dve', lambda: V.scalar_tensor_tensor(out=xT[:, m, :], in0=ps[pbk][:], scalar=modT[:, l, 32 + m, h:h + 1], in1=xT[:, m, :],
                                                                    op0=ALU.mult, op1=ALU.add),
                              reads=[('ps', pbk), 'modT', ('C', 'xT', m)], writes=[('C', 'xT', m)])
                        tr.dma('sp', xT_s[m, :, t0:t0 + 512], xT[:, m, :], reads=[('C', 'xT', m)], writes=['xT_s'])
                tr.barrier()

            with contextlib.ExitStack() as sd:
                xT = sb(sd, "D_xT", [128, KC, 512], F32)
                hT = sb(sd, "D_hT", [128, KC, 512], BF16)
                hid = sb(sd, "D_hid", [128, 64, 512], BF16)
                sq = [sb(sd, f"D_sq{i}", [128, 512], BF16) for i in range(2)]
                sq_f = [sb(sd, f"D_sqf{i}", [128, 512], F32) for i in range(2)]
                rstd = sb(sd, "D_rstd", [128, 512], F32)
                rl = [sb(sd, f"D_rl{i}", [128, 512], F32) for i in range(2)]
                yo = [sb(sd, f"D_yo{i}", [128, D], F32) for i in range(2)]
                for blk in range(NB if PH >= 4 else 0):
                    h = 0 if blk < HB else 1
                    t0 = blk * 512
                    for kc in range(KC):
                        tr.dma('sp', xT[:, kc, :], xT_s[kc, :, t0:t0 + 512], writes=[('D', 'xT', kc)])
                    norm_mod(xT, hT, sq, rstd, GS[:, 1, h, 0, :], GS[:, 1, h, 1, :], 0, 'D')
                    for jf in range(64):
                        wi = wload(w1_in[l, jf])
                        pbk = 1 + jf % 3
                        for kc in range(KC):
                            tr.op('pe', lambda kc=kc: P.matmul(ps[pbk][:], lhsT=wr[wi][:, kc, :], rhs=hT[:, kc, :], start=(kc == 0), stop=(kc == KC - 1)),
                                  reads=[('wr', wi), ('D', 'hT', kc)], writes=[('ps', pbk)])
                        rb = rl[jf % 2]
                        tr.op('act', lambda: S.activation(out=rb[:], in_=ps[pbk][:], func=AF.Relu), reads=[('ps', pbk)], writes=[('rl', jf % 2)])
                        tr.op('dve', lambda: V.tensor_tensor(out=hid[:, jf, :], in0=rb[:], in1=rb[:], op=ALU.mult),
                              reads=[('rl', jf % 2)], writes=[('hid', jf)])
                    for m in range(16):
                        pbk = 4 + m % 2
                        for q4 in range(4):
                            wi = wload(w2m_in[l, m * 4 + q4])
                            for jq in range(16):
                                jf = q4 * 16 + jq
                                tr.op('pe', lambda jq=jq, jf=jf: P.matmul(ps[pbk][:], lhsT=wr[wi][:, jq, :], rhs=hid[:, jf, :],
                                                                          start=(jf == 0), stop=(jf == 63)),
                                      reads=[('wr', wi), ('hid', jf)], writes=[('ps', pbk)])
                        tr.op('dve', lambda: V.scalar_tensor_tensor(out=xT[:, m, :], in0=ps[pbk][:], scalar=modT[:, l, 80 + m, h:h + 1], in1=xT[:, m, :],
                                                                    op0=ALU.mult, op1=ALU.add),
                              reads=[('ps', pbk), 'modT', ('D', 'xT', m)], writes=[('D', 'xT', m)])
                        if l < DEPTH - 1:
                            tr.dma('sp', xT_s[m, :, t0:t0 + 512], xT[:, m, :], reads=[('D', 'xT', m)], writes=['xT_s'])
                    if l == DEPTH - 1:
                        norm_mod(xT, None, sq, rstd, None, None, 0, 'D')
                        for q in range(4):
                            ob = yo[q % 2]
                            for kc in range(KC):
                                b = kc % 2
                                tr.op('dve', lambda: V.tensor_tensor(out=sq_f[b][:, 0:128], in0=xT[:, kc, q * 128:(q + 1) * 128],
                                                                     in1=rstd[:, q * 128:(q + 1) * 128], op=ALU.mult),
                                      reads=[('D', 'xT', kc), ('D', 'rstd')], writes=[('D', 'sqf', b)])
                                tr.op('act', lambda: S.activation(out=sq_f[b][:, 128:256], in_=sq_f[b][:, 0:128], func=AF.Copy, scale=fgt[:, kc:kc + 1]),
                                      reads=[('D', 'sqf', b), 'fgt'], writes=[('D', 'sqf2', b)])
                                pbk = 6 + (kc // 4) % 2
                                tr.op('pe', lambda: P.transpose(out=ps[pbk][:, (kc % 4) * 128:(kc % 4 + 1) * 128], in_=sq_f[b][:, 128:256], identity=ident_f),
                                      reads=[('D', 'sqf2', b), 'cf'], writes=[('ps', pbk)])
                                if kc % 4 == 3:
                                    g4 = kc // 4
                                    tr.op('act', lambda: S.copy(out=ob[:, g4 * 512:(g4 + 1) * 512], in_=ps[pbk][:]), reads=[('ps', pbk)],
                                          writes=[('yo', q % 2, g4)])
                            tr.dma('sp', y_out[t0 + q * 128:t0 + (q + 1) * 128, :], ob[:], reads=[('yo', q % 2, g4) for g4 in range(4)], writes=['y_out'])
                tr.barrier()
        tr.barrier()
    return nc, tr


def _tiles(W, ncols_pad=None):
    K, N = W.shape
    return np.ascontiguousarray(W.reshape(K // 128, 128, N // 128, 128).transpose(2, 1, 0, 3)).reshape(N // 128, 128, (K // 128) * 128)


def _fm(v, n):
    sh = v.shape[:-1]
    a = v.reshape(*sh, n, 128)
    return np.ascontiguousarray(np.moveaxis(a, -1, 0))


def prep_shared(inp, DEPTH):
    f32 = np.float32
    o = {}
    ident = np.eye(128, dtype=f32)
    bones = np.kron(np.eye(2, dtype=f32), np.ones((64, 64), f32))
    mones = np.full((128, 128), 1.0 / 2048, f32)
    o['consts'] = np.ascontiguousarray(np.stack([ident, bones, mones], axis=1))
    r = np.arange(128)[:, None]
    c = np.arange(128)[None, :]
    Us, Ls, Ui, Li = (r < c), (r > c), (r <= c), (r >= c)
    m6 = [Us, Ls, Ui, Ls, Us, Li]
    o['masks'] = np.ascontiguousarray(np.stack([np.tile(m.astype(f32), (1, 4)) for m in m6], axis=1))
    o['ada_w'] = np.stack([_tiles(inp['ada_w'][l]) for l in range(DEPTH)])
    o['ada_b'] = np.ascontiguousarray(_fm(inp['ada_b'][:DEPTH], 96))
    o['gains'] = np.ascontiguousarray(np.stack([_fm(inp['norm1_g'][:DEPTH], 16), _fm(inp['norm2_g'][:DEPTH], 16)], axis=2))
    o['final_g'] = np.ascontiguousarray(_fm(inp['final_g'], 16))
    win = []
    for l in range(DEPTH):
        W = inp['w_in'][l]
        Wp = np.zeros((2048, 4608), f32)
        Wp[:, 0:3488] = W[:, 0:3488]
        Wp[:, 3584:4608] = W[:, 3488:4512]
        win.append(_tiles(Wp))
    o['w_in'] = np.stack(win)
    kinds = [inp['k_k'], inp['k_a'], inp['r_k'].reshape(-1, 1024), inp['lnx_w'], inp['lnx_b'], inp['pool_scale'],
             inp['w0'][:, 0], inp['w0'][:, 1], inp['a0'][:, 0], inp['a0'][:, 1]]
    o['vecs'] = np.ascontiguousarray(np.stack([_fm(k[:DEPTH], 8) for k in kinds], axis=2))
    for nm in ('w2', 'a2'):
        z = np.zeros((DEPTH, 2, 64, 2, 1024), f32)
        for d in range(2):
            z[:, d, :, d, :] = inp[nm][:DEPTH, d]
        o[nm] = np.ascontiguousarray(z.reshape(DEPTH, 128, 2048))
    g2p = np.zeros((DEPTH, 256, 1024), f32)
    g2p[:, :160] = inp['g2'][:DEPTH]
    o['g2'] = np.ascontiguousarray(g2p.reshape(DEPTH, 2, 128, 1024).transpose(0, 2, 1, 3)).reshape(DEPTH, 128, 2048)
    pw = inp['pool_w'][:DEPTH].reshape(DEPTH, 4, 2, 128, 256)
    o['pool_w'] = np.ascontiguousarray(pw.transpose(0, 3, 1, 2, 4)).reshape(DEPTH, 128, 2048)
    o['w_out'] = np.stack([_tiles(inp['w_out'][l]) for l in range(DEPTH)])
    o['w1'] = np.stack([_tiles(inp['mlp_w1'][l]) for l in range(DEPTH)])
    w2m = []
    for l in range(DEPTH):
        t = _tiles(inp['mlp_w2'][l]).reshape(16, 128, 4, 16 * 128)
        w2m.append(np.ascontiguousarray(t.transpose(0, 2, 1, 3)).reshape(64, 128, 2048))
    o['w2m'] = np.stack(w2m)
    return o


def prep_core(xs, cs, carry, T):
    f32 = np.float32
    x = np.ascontiguousarray(np.concatenate(xs, axis=0))
    assert x.shape[0] == T
    c2 = np.stack([cs[0], cs[-1]], axis=0)
    cT = np.ascontiguousarray(c2.reshape(2, 16, 128).transpose(2, 1, 0))
    cm = np.full((128, 1), 1.0 if carry else 0.0, f32)
    rc = np.zeros((4, T), f32)
    segs = [(0, T)] if carry else [(0, T // 2), (T // 2, T)]
    for gi, w in enumerate((2, 4, 8, 16)):
        for (a, b) in segs:
            t = np.arange(a, b)
            lo = np.clip(t - w // 2, a, b)
            hi = np.clip(t + w // 2, a, b)
            rc[gi, a:b] = 1.0 / (hi - lo).astype(f32)
    return {'x': x, 'cT': cT, 'cm': cm, 'rc': rc}


_CACHE = {}


def run(inp, NCH, DEPTH, seqs, ncores=8):
    T = NCH * 128
    key = (NCH, DEPTH)
    if key not in _CACHE:
        _CACHE[key] = build(NCH, DEPTH)[0]
    nc = _CACHE[key]
    shared = prep_shared(inp, DEPTH)
    in_maps = []
    for i in range(ncores):
        xs, cs, carry = seqs[i] if i < len(seqs) else seqs[0]
        m = dict(shared)
        m.update(prep_core(xs, cs, carry, T))
        in_maps.append(m)
    if os.environ.get('KTRACE') == '1':
        res = run_bass_kernel_spmd(nc, in_maps, core_ids=list(range(ncores)), trace=True)
        print("EXEC_NS", res.exec_time_ns)
    else:
        res = run_bass_kernel_spmd(nc, in_maps, core_ids=list(range(ncores)))
    return [res.results[i]['y'] for i in range(len(seqs))]


def kernel(**inp):
    inp = {k: np.asarray(v) for k, v in inp.items()}
    xp, xs_, cp, cs_ = inp['x_prompt'], inp['x_sample'], inp['c_prompt'], inp['c_sample']
    seqs = [([xp[0]], [cp[0]], True), ([xp[1]], [cp[1]], True), ([xs_[0], xs_[1]], [cs_[0], cs_[1]], False)]
    ys = run(inp, 64, 4, seqs, ncores=3)
    y_prompt = np.stack([ys[0], ys[1]], axis=0)
    y_sample = np.stack([ys[2][:4096], ys[2][4096:]], axis=0)
    return (y_prompt.astype(np.float32), y_sample.astype(np.float32))
```
